# Optimizing a Trainium2 kernel written in Bass

```python
import jax, jax.numpy as jnp
from jax import lax
import numpy as np

D_MODEL = 2048
BATCH = 16
SEQ = 2048
DEPTH = 2

D_MIX = D_MODEL
GROUP_DIM = 128
N_GROUPS = D_MIX // GROUP_DIM
MLA_WIDTH = D_MIX // 2
V_HEAD_DIM = 128
MLA_HEADS = MLA_WIDTH // V_HEAD_DIM
QK_NOPE_DIM = 128
QK_ROPE_DIM = 64
Q_LORA_RANK = 512
KV_LORA_RANK = 512
ROPE_THETA = 10000.0
Q_BLOCK = 128
CONF_WIDTH = D_MIX // 4
CONF_KERNEL = 31
SC_WIDTH = D_MIX - MLA_WIDTH - CONF_WIDTH
SC_KERNEL = 3
D_FF = 256 * ((8 * D_MODEL // 3 + 255) // 256)
IN_WIDTH = Q_LORA_RANK + KV_LORA_RANK + QK_ROPE_DIM + 2 * CONF_WIDTH + 3 * SC_WIDTH
RMS_EPS = 1e-6
LN_EPS = 1e-5

kernel_name = "hymba_mla_conformer_shortconv_macaron"


def rms_norm(x, g):
    xf = x.astype(jnp.float32)
    y = xf * lax.rsqrt(jnp.mean(xf * xf, axis=-1, keepdims=True) + RMS_EPS)
    return (y * g.astype(jnp.float32)).astype(x.dtype)


def layer_norm(x, g, b):
    xf = x.astype(jnp.float32)
    mu = jnp.mean(xf, axis=-1, keepdims=True)
    d = xf - mu
    var = jnp.mean(d * d, axis=-1, keepdims=True)
    return (d * lax.rsqrt(var + LN_EPS) * g.astype(jnp.float32) + b.astype(jnp.float32)).astype(x.dtype)


def swiglu(x, w_gate, w_up, w_down):
    return (jax.nn.silu(x @ w_gate) * (x @ w_up)) @ w_down


def causal_depthwise_conv(x, w):
    k, c = w.shape
    return lax.conv_general_dilated(
        x, w[:, None, :], window_strides=(1,), padding=[(k - 1, 0)],
        dimension_numbers=("NWC", "WIO", "NWC"), feature_group_count=c)


def rope_tables(seq):
    inv = 1.0 / (ROPE_THETA ** (jnp.arange(0, QK_ROPE_DIM, 2, dtype=jnp.float32) / QK_ROPE_DIM))
    ang = jnp.arange(seq, dtype=jnp.float32)[:, None] * inv[None, :]
    return jnp.cos(ang), jnp.sin(ang)


def apply_rope(x, cos, sin):
    xf = x.astype(jnp.float32)
    half = xf.shape[-1] // 2
    x1, x2 = xf[..., :half], xf[..., half:]
    return jnp.concatenate([x1 * cos - x2 * sin, x2 * cos + x1 * sin], axis=-1).astype(x.dtype)


def mla(c_q, c_kv, k_rope, q_norm, w_uq, kv_norm, w_ukv, cos, sin):
    b, s, _ = c_q.shape
    q = (rms_norm(c_q, q_norm) @ w_uq).reshape(b, s, MLA_HEADS, QK_NOPE_DIM + QK_ROPE_DIM)
    q_nope = q[..., :QK_NOPE_DIM]
    q_rope = apply_rope(q[..., QK_NOPE_DIM:], cos[:, None, :], sin[:, None, :])
    k_rope = apply_rope(k_rope, cos, sin)
    kv = (rms_norm(c_kv, kv_norm) @ w_ukv).reshape(b, s, MLA_HEADS, QK_NOPE_DIM + V_HEAD_DIM)
    k_nope, v = kv[..., :QK_NOPE_DIM], kv[..., QK_NOPE_DIM:]
    scale = (QK_NOPE_DIM + QK_ROPE_DIM) ** -0.5
    outs = []
    for i in range(s // Q_BLOCK):
        q0, q1 = i * Q_BLOCK, (i + 1) * Q_BLOCK
        sc = (jnp.einsum("bqhd,bkhd->bhqk", q_nope[:, q0:q1], k_nope[:, :q1])
              + jnp.einsum("bqhr,bkr->bhqk", q_rope[:, q0:q1], k_rope[:, :q1]))
        sc = sc.astype(jnp.float32) * scale
        qpos = q0 + jnp.arange(Q_BLOCK)[:, None]
        kpos = jnp.arange(q1)[None, :]
        sc = jnp.where(kpos <= qpos, sc, -jnp.inf)
        p = jax.nn.softmax(sc, axis=-1).astype(v.dtype)
        outs.append(jnp.einsum("bhqk,bkhd->bqhd", p, v[:, :q1]))
    o = jnp.concatenate(outs, axis=1)
    return o.reshape(b, s, MLA_HEADS * V_HEAD_DIM)


def conformer_conv(a, g, dw_w, dw_b, ln_g, ln_b):
    u = a * jax.nn.sigmoid(g)
    u = causal_depthwise_conv(u, dw_w) + dw_b
    u = layer_norm(u, ln_g, ln_b)
    return jax.nn.silu(u)


def short_conv(b_gate, c_gate, xs, w):
    return b_gate * causal_depthwise_conv(c_gate * xs, w)


def setup_inputs(seed: int = 0) -> dict:
    key = jax.random.key(seed)
    ks = jax.random.split(key, 24)
    f32 = jnp.float32
    L = DEPTH

    def w(k, shape, fan_in):
        return jax.random.normal(k, shape, f32) * (fan_in ** -0.5)

    def gain(k, shape):
        return 1.0 + 0.02 * jax.random.normal(k, shape, f32)

    def small(k, shape):
        return 0.02 * jax.random.normal(k, shape, f32)

    return {
        "x": jax.random.normal(ks[0], (BATCH, SEQ, D_MODEL), f32),
        "ffn1_norm": gain(ks[1], (L, D_MODEL)),
        "ffn1_w_gate": w(ks[2], (L, D_MODEL, D_FF), D_MODEL),
        "ffn1_w_up": w(ks[3], (L, D_MODEL, D_FF), D_MODEL),
        "ffn1_w_down": w(ks[4], (L, D_FF, D_MODEL), D_FF),
        "mix_norm": gain(ks[5], (L, D_MODEL)),
        "w_in": w(ks[6], (L, D_MODEL, IN_WIDTH), D_MODEL),
        "q_norm": gain(ks[7], (L, Q_LORA_RANK)),
        "w_uq": w(ks[8], (L, Q_LORA_RANK, MLA_HEADS * (QK_NOPE_DIM + QK_ROPE_DIM)), Q_LORA_RANK),
        "kv_norm": gain(ks[9], (L, KV_LORA_RANK)),
        "w_ukv": w(ks[10], (L, KV_LORA_RANK, MLA_HEADS * (QK_NOPE_DIM + V_HEAD_DIM)), KV_LORA_RANK),
        "conf_dw_w": w(ks[11], (L, CONF_KERNEL, CONF_WIDTH), CONF_KERNEL),
        "conf_dw_b": small(ks[12], (L, CONF_WIDTH)),
        "conf_ln_g": gain(ks[13], (L, CONF_WIDTH)),
        "conf_ln_b": small(ks[14], (L, CONF_WIDTH)),
        "sc_w": w(ks[15], (L, SC_KERNEL, SC_WIDTH), SC_KERNEL),
        "mix_out_norm": gain(ks[16], (L, N_GROUPS, GROUP_DIM)),
        "w_o": w(ks[17], (L, D_MIX, D_MODEL), D_MIX),
        "ffn2_norm": gain(ks[18], (L, D_MODEL)),
        "ffn2_w_gate": w(ks[19], (L, D_MODEL, D_FF), D_MODEL),
        "ffn2_w_up": w(ks[20], (L, D_MODEL, D_FF), D_MODEL),
        "ffn2_w_down": w(ks[21], (L, D_FF, D_MODEL), D_FF),
        "final_norm": gain(ks[22], (D_MODEL,)),
    }


def reference(x, ffn1_norm, ffn1_w_gate, ffn1_w_up, ffn1_w_down, mix_norm, w_in,
              q_norm, w_uq, kv_norm, w_ukv, conf_dw_w, conf_dw_b, conf_ln_g, conf_ln_b,
              sc_w, mix_out_norm, w_o, ffn2_norm, ffn2_w_gate, ffn2_w_up, ffn2_w_down,
              final_norm):
    b, s, _ = x.shape
    cos, sin = rope_tables(s)
    widths = [Q_LORA_RANK, KV_LORA_RANK, QK_ROPE_DIM, CONF_WIDTH, CONF_WIDTH, SC_WIDTH, SC_WIDTH, SC_WIDTH]
    split_at = [int(v) for v in np.cumsum(widths)[:-1]]
    h = x
    for l in range(DEPTH):
        h = h + 0.5 * swiglu(rms_norm(h, ffn1_norm[l]), ffn1_w_gate[l], ffn1_w_up[l], ffn1_w_down[l])
        z = rms_norm(h, mix_norm[l]) @ w_in[l]
        c_q, c_kv, k_rope, conf_a, conf_g, sc_b, sc_c, sc_x = jnp.split(z, split_at, axis=-1)
        o_mla = mla(c_q, c_kv, k_rope, q_norm[l], w_uq[l], kv_norm[l], w_ukv[l], cos, sin)
        o_conf = conformer_conv(conf_a, conf_g, conf_dw_w[l], conf_dw_b[l], conf_ln_g[l], conf_ln_b[l])
        o_sc = short_conv(sc_b, sc_c, sc_x, sc_w[l])
        y = jnp.concatenate([o_mla, o_conf, o_sc], axis=-1).reshape(b, s, N_GROUPS, GROUP_DIM)
        y = rms_norm(y, mix_out_norm[l]).reshape(b, s, D_MIX)
        h = h + y @ w_o[l]
        h = h + 0.5 * swiglu(rms_norm(h, ffn2_norm[l]), ffn2_w_gate[l], ffn2_w_up[l], ffn2_w_down[l])
    return rms_norm(h, final_norm)
```

```python
import numpy as np
from contextlib import ExitStack
import concourse.bass as bass
import concourse.mybir as mybir
from concourse.bass_utils import run_bass_kernel_spmd

F32 = mybir.dt.float32
BF16 = mybir.dt.bfloat16
ALU = mybir.AluOpType
AF = mybir.ActivationFunctionType

D = 2048
KC = 16
DFF = 5632
FC = 44
NPART = 4
FPP = 11
T = 512
TM = 256
H = 8
NS = 6
RMS_EPS = 1e-6
LN_EPS = 1e-5
SCALE = float(192 ** -0.5)
NPL = 220

KINDS = {
    'g1': (44, 2048), 'u1': (44, 2048), 'd1': (64, 1408),
    'win': (29, 2048), 'uq': (4, 2048), 'ukv': (4, 2048), 'wo': (16, 2048),
    'g2': (44, 2048), 'u2': (44, 2048), 'd2': (64, 1408),
}
CAST_ORDER = ['g1', 'u1', 'd1', 'win', 'uq', 'ukv', 'wo', 'g2', 'u2', 'd2']


class Sched:
    ENG = ('pe', 'act', 'dve', 'pool')

    def __init__(self, nc, stack):
        self.nc = nc
        self.stack = stack
        self.h = dict(pe=nc.tensor, act=nc.scalar, dve=nc.vector, pool=nc.gpsimd, sp=nc.sync)
        self.cnt = {}
        self.sem = {}
        self.semkey = {}
        self.seen = {e: {} for e in self.h}
        self.res = {}
        self.nsem = 0
        self.dsem = {}
        self.ninst = 0
        self.nwait = 0
        self.new_epoch()

    def _newsem(self, name):
        s = self.stack.enter_context(self.nc.semaphore(name))
        self.nsem += 1
        return s

    def new_epoch(self):
        for e in self.ENG:
            nm = "%s_e%d" % (e, self.nsem)
            self.sem[e] = self._newsem(nm)
            self.semkey[e] = nm
            self.cnt[e] = 0

    def _wait(self, eng, stamp):
        sem, key, val, peng = stamp
        if peng == eng:
            if eng == 'pe':
                return
        if self.seen[eng].get(key, 0) >= val:
            return
        self.h[eng].wait_ge(sem, val)
        self.nwait += 1
        self.seen[eng][key] = val

    def _deps(self, eng, reads, writes):
        for r in reads:
            st = self.res.get(r)
            if st is not None and st[0] is not None:
                self._wait(eng, st[0])
        for w in writes:
            st = self.res.get(w)
            if st is not None:
                if st[0] is not None:
                    self._wait(eng, st[0])
                for s in st[1].values():
                    self._wait(eng, s)

    def _commit(self, stamp, rkey, reads, writes):
        for r in reads:
            st = self.res.get(r)
            if st is None:
                st = [None, {}]
                self.res[r] = st
            st[1][rkey] = stamp
        for w in writes:
            self.res[w] = [stamp, {}]

    def _stamp(self, eng):
        return (self.sem[eng], self.semkey[eng], self.cnt[eng], eng)

    def op(self, eng, name, kw, reads=(), writes=()):
        self._deps(eng, reads, writes)
        ins = getattr(self.h[eng], name)(**kw)
        self.cnt[eng] += 1
        ins.then_inc(self.sem[eng], 1)
        self.ninst += 1
        self._commit(self._stamp(eng), eng, reads, writes)

    def mm(self, out, pairs, reads=(), writes=(), start=True, stop=True):
        self._deps('pe', reads, writes)
        n = len(pairs)
        ins = None
        for i, (l, r) in enumerate(pairs):
            ins = self.nc.tensor.matmul(out, l, r, start=(start and i == 0), stop=(stop and i == n - 1))
        self.cnt['pe'] += 1
        ins.then_inc(self.sem['pe'], 1)
        self.ninst += n
        self._commit(self._stamp('pe'), 'pe', reads, writes)

    def mm_multi(self, groups, reads=(), writes=()):
        self._deps('pe', reads, writes)
        ins = None
        for out, pairs, start, stop in groups:
            n = len(pairs)
            for i, (l, r) in enumerate(pairs):
                ins = self.nc.tensor.matmul(out, l, r, start=(start and i == 0), stop=(stop and i == n - 1))
                self.ninst += 1
        self.cnt['pe'] += 1
        ins.then_inc(self.sem['pe'], 1)
        self._commit(self._stamp('pe'), 'pe', reads, writes)

    def dma(self, q, out, in_, reads=(), writes=(), semname='d', serialize=True, commit=True):
        ds = self.dsem.get(semname)
        if ds is None:
            ds = [self._newsem(semname), 0]
            self.dsem[semname] = ds
        if serialize and ds[1] > 0:
            self._wait(q, (ds[0], semname, ds[1], 'dma'))
        self._deps(q, reads, writes)
        ins = self.h[q].dma_start(out=out, in_=in_)
        ds[1] += 16
        ins.then_inc(ds[0], 16)
        self.ninst += 1
        stamp = (ds[0], semname, ds[1], 'dma')
        if commit:
            self._commit(stamp, 'dma:' + semname, reads, writes)
        return stamp

    def barrier(self, extra=()):
        engs = ('pe', 'act', 'dve')
        stamps = {e: self._stamp(e) for e in engs}
        for e in engs + tuple(extra):
            for e2 in engs:
                if e2 != e and stamps[e2][2] > 0:
                    self._wait(e, stamps[e2])

    def wait_all_dma(self, eng):
        for name, ds in self.dsem.items():
            if ds[1] > 0:
                self._wait(eng, (ds[0], name, ds[1], 'dma'))


def build(nL, nS, S):
    nc = bass.Bass("TRN2", target_bir_lowering=False)
    NT = S // T
    NKB = S // 128
    NPAR = nL * NPL + 16

    xT = nc.dram_tensor("xT", [nS * D, S], F32, kind="ExternalInput").ap()
    outT = nc.dram_tensor("outT", [nS * D, S], F32, kind="ExternalOutput").ap()
    hscr = nc.dram_tensor("hscr", [nS * D, S], F32, kind="Internal").ap()
    wsrc, wdst = {}, {}
    for k, (nsl, W) in KINDS.items():
        wsrc[k] = nc.dram_tensor("w_" + k, [nL * nsl * 128, W], F32, kind="ExternalInput").ap()
        wdst[k] = nc.dram_tensor("b_" + k, [nL * nsl * 128, W], BF16, kind="Internal").ap()
    par_d = nc.dram_tensor("par", [128, NPAR], F32, kind="ExternalInput").ap()
    cst_d = nc.dram_tensor("cst", [128, S], F32, kind="ExternalInput").ap()
    msk_d = nc.dram_tensor("msk", [128, 128], F32, kind="ExternalInput").ap()
    dmat_d = nc.dram_tensor("dmat", [128, 128], F32, kind="ExternalInput").ap()

    with ExitStack() as stack:
        def sb(name, shape, dt):
            return stack.enter_context(nc.sbuf_tensor(name, shape, dt))

        hT = sb("hT", [128, KC, T], F32)
        xn = sb("xn", [128, KC, T], BF16)
        slots = sb("slots", [128, NS, 2048], BF16)
        kT = sb("kT", [128, H, S], BF16)
        krT = sb("krT", [128, S], BF16)
        Vc = sb("Vc", [128, NKB, H * 128], BF16)
        sqb = sb("sqb", [128, 4, T], BF16)
        lnt = sb("lnt", [128, 2, T], F32)
        rstd = sb("rstd", [128, 2, T], F32)
        par = sb("par_sb", [128, NPAR], F32)
        ones = sb("ones", [128, 128], BF16)
        mask = sb("mask", [128, 128], BF16)
        dmat = sb("dmat_sb", [128, 128], F32)
        UW = 13312
        U = sb("U", [128, UW], F32)
        ps = stack.enter_context(nc.psum_tensor("ps", [128, 8, 512], F32))

        actb = U[:, 0:5632].bitcast(BF16).rearrange("p (a b c) -> p a b c", a=2, b=FPP)
        sgt = U[:, 5632:6656].rearrange("p (a b) -> p a b", b=T)
        stg = U[:, 6656:10752].rearrange("p (a b) -> p a b", b=1024)
        off = [0]

        def carve(ncols_f32):
            a = off[0]
            off[0] += ncols_f32
            assert off[0] <= UW, off[0]
            return U[:, a:a + ncols_f32]

        cs = carve(TM)
        c32 = carve(4 * TM).rearrange("p (a b) -> p a b", b=TM)
        cn = carve(4 * TM).bitcast(BF16).rearrange("p (a b c) -> p a b c", a=2, b=4)
        qT = carve(4 * TM).bitcast(BF16).rearrange("p (a b) -> p a b", b=TM)
        qrT = carve(4 * TM).bitcast(BF16).rearrange("p (a b) -> p a b", b=TM)
        pT2 = carve(2 * TM).bitcast(BF16).rearrange("p (a b) -> p a b", b=2 * TM)
        yT = carve(8 * TM).bitcast(BF16).rearrange("p (a b) -> p a b", b=TM)
        acc = carve(4 * TM).rearrange("p (a b) -> p a b", b=TM)
        vb = carve(2 * TM).bitcast(BF16).rearrange("p (a b) -> p a b", b=TM)
        NTMP = 8
        tmpt = carve(NTMP * TM).rearrange("p (a b) -> p a b", b=TM)
        rstdl = carve(TM)
        assert off[0] <= 10752, off[0]
        off[0] = 10752
        ubuf = carve(4 * (TM + 32)).rearrange("p (a b) -> p a b", b=TM + 32)
        mbuf = carve(4 * (TM + 4)).rearrange("p (a b) -> p a b", b=TM + 4)
        mean = carve(TM)

        sch = Sched(nc, stack)

        rot = {'A': 0, 'S': 0, 'P': 0, 'T': 0, 'Q': 0, 'R': 0, 'slot': 0, 'G': 0, 'S2': 0}

        mixstate = {'abanks': 4}

        ABANKS = {4: (0, 1, 2, 3), 2: (0, 1), 6: (0, 1, 2, 3, 4, 5)}

        def rotA():
            lst = ABANKS[mixstate['abanks']]
            b = lst[rot['A'] % len(lst)]
            rot['A'] += 1
            return b

        def chk_tmp(mark):
            assert rot['T'] - mark < NTMP - 1, (rot['T'], mark)

        def rotS():
            b = 4 + rot['S'] % 2
            rot['S'] += 1
            return b

        def rotP():
            b = rot['P'] % 2
            rot['P'] += 1
            return b

        def tmp():
            i = rot['T'] % NTMP
            rot['T'] += 1
            return tmpt[:, i, :], ('tmp', i)

        def rotR():
            i = rot['R'] % 2
            rot['R'] += 1
            return i

        def rotQ():
            i = rot['Q'] % 4
            rot['Q'] += 1
            return i

        def act(out, in_, func, reads, writes, **kw):
            sch.op('act', 'activation', dict(out=out, in_=in_, func=func, **kw), reads, writes)

        def tt(eng, out, in0, in1, op, reads, writes):
            sch.op(eng, 'tensor_tensor', dict(out=out, in0=in0, in1=in1, op=op), reads, writes)

        def stt(out, in0, scalar, in1, op0, op1, reads, writes):
            sch.op('dve', 'scalar_tensor_tensor',
                   dict(out=out, in0=in0, scalar=scalar, in1=in1, op0=op0, op1=op1), reads, writes)

        def ts(eng, out, in0, s1, s2, op0, op1, reads, writes):
            kw = dict(out=out, in0=in0, scalar1=s1, scalar2=s2, op0=op0)
            if s2 is not None:
                kw['op1'] = op1
            sch.op(eng, 'tensor_scalar', kw, reads, writes)

        def pc(c):
            return par[:, c:c + 1]

        def load_slot_plain(kind, l, idx):
            nsl, W = KINDS[kind]
            si = rot['slot'] % NS
            rot['slot'] += 1
            r0 = (l * nsl + idx) * 128
            if (kind, l, idx) in store_stamp:
                sch._wait('sp', store_stamp[(kind, l, idx)])
            else:
                sch._wait('sp', cast_stamp[(kind, l, (idx * 128) // 2048)])
            sch.dma('sp', out=slots[:, si, 0:W], in_=wdst[kind][r0:r0 + 128, :],
                    reads=[], writes=[('slot', si)], semname="slot%d" % si)
            return si

        store_stamp = {}
        pend_wb = []
        STAGED = ('g1', 'u1', 'd1', 'g2', 'u2', 'd2')

        def flush_wb(keep):
            while len(pend_wb) > keep:
                kind, l, idx, si, W, r0 = pend_wb.pop(0)
                store_stamp[(kind, l, idx)] = sch.dma(
                    'sp', out=wdst[kind][r0:r0 + 128, :], in_=slots[:, si, 0:W],
                    reads=[('slot', si)], writes=[], semname="wb%d" % (rot['G'] % 3))
                rot['G'] += 1

        def load_slot_staged(kind, l, idx):
            nsl, W = KINDS[kind]
            si = rot['slot'] % NS
            rot['slot'] += 1
            r0 = (l * nsl + idx) * 128
            hw = W // 2
            for hf in range(2):
                gi = rot['S2'] % 4
                rot['S2'] += 1
                c0_ = hf * hw
                sch.dma('sp', out=stg[:, gi, 0:hw], in_=wsrc[kind][r0:r0 + 128, c0_:c0_ + hw], reads=[],
                        writes=[('stg', gi)], semname="stg%d" % gi)
                act(slots[:, si, c0_:c0_ + hw], stg[:, gi, 0:hw], AF.Copy, [('stg', gi)], [('slot', si)])
            pend_wb.append((kind, l, idx, si, W, r0))
            flush_wb(2)
            return si

        ncast = [0]
        cast_stamp = {}

        def cast_dma(kind, l, di):
            nsl, W = KINDS[kind]
            R = nsl * 128
            base = l * R
            r0 = di * 2048
            n = min(2048, R - r0)
            ncast[0] += 1
            cast_stamp[(kind, l, di)] = sch.dma(
                'pool', out=wdst[kind][base + r0:base + r0 + n, :], in_=wsrc[kind][base + r0:base + r0 + n, :],
                semname="cast%d" % (ncast[0] % 7), serialize=True, commit=False)

        def ncd(kind):
            return (KINDS[kind][0] * 128 + 2047) // 2048

        def cast_weights(l):
            for kind in ('win', 'uq', 'ukv', 'wo'):
                for di in range(ncd(kind)):
                    cast_dma(kind, l, di)
            return
            for di in range(ncd('g1')):
                cast_dma('g1', l, di)
                cast_dma('u1', l, di)
            for di in range(ncd('d1')):
                cast_dma('d1', l, di)
            for kind in ('win', 'uq', 'ukv', 'wo'):
                for di in range(ncd(kind)):
                    cast_dma(kind, l, di)
            for di in range(ncd('g2')):
                cast_dma('g2', l, di)
                cast_dma('u2', l, di)
            for di in range(ncd('d2')):
                cast_dma('d2', l, di)

        def rstd_from(src_ap, src_key, ncols, scale, eps):
            i = rotR()
            act(lnt[:, i, :ncols], src_ap, AF.Ln, [src_key], [('lnt', i)], scale=scale, bias=eps)
            act(rstd[:, i, :ncols], lnt[:, i, :ncols], AF.Exp, [('lnt', i)], [('rstd', i)], scale=-0.5)
            return rstd[:, i, :ncols], ('rstd', i)

        def rmsnorm_D(gcol):
            for kc in range(KC):
                qi = rotQ()
                act(sqb[:, qi, :], hT[:, kc, :], AF.Square, [('h', kc)], [('sqb', qi)])
                sch.mm(ps[:, 6, :], [(ones[:, :], sqb[:, qi, :])], reads=[('sqb', qi), 'ones'],
                       writes=[('ps', 6)], start=(kc == 0), stop=(kc == KC - 1))
            r_ap, r_key = rstd_from(ps[:, 6, :], ('ps', 6), T, 1.0 / D, RMS_EPS)
            for kc in range(KC):
                stt(xn[:, kc, :], hT[:, kc, :], pc(gcol + kc), r_ap, ALU.mult, ALU.mult,
                    [('h', kc), r_key, 'par'], [('xn', kc)])

        def group_norm(src_ap, src_key, ycol, gcol):
            qi = rotQ()
            act(sqb[:, qi, :TM], src_ap, AF.Square, [src_key], [('sqb', qi)])
            b = rotA()
            sch.mm(ps[:, b, :TM], [(ones[:, :], sqb[:, qi, :TM])], reads=[('sqb', qi), 'ones'], writes=[('ps', b)])
            r_ap, r_key = rstd_from(ps[:, b, :TM], ('ps', b), TM, 1.0 / 128, RMS_EPS)
            stt(yT[:, ycol, :], src_ap, pc(gcol), r_ap, ALU.mult, ALU.mult,
                [src_key, r_key, 'par'], [('y', ycol)])

        xkeys = [('xn', kc) for kc in range(KC)]

        def ffn(l, which, staged=False):
            gk, uk, dk = ('g1', 'u1', 'd1') if which == 1 else ('g2', 'u2', 'd2')
            load_slot = load_slot_staged if staged else load_slot_plain

            def GU(q):
                for i in range(FPP):
                    f = q * FPP + i
                    sg_ = load_slot(gk, l, f)
                    bg = rotA()
                    sch.mm(ps[:, bg, :], [(slots[:, sg_, kc * 128:(kc + 1) * 128], xn[:, kc, :]) for kc in range(KC)],
                           reads=[('slot', sg_)] + xkeys, writes=[('ps', bg)])
                    su_ = load_slot(uk, l, f)
                    bu = rotA()
                    sch.mm(ps[:, bu, :], [(slots[:, su_, kc * 128:(kc + 1) * 128], xn[:, kc, :]) for kc in range(KC)],
                           reads=[('slot', su_)] + xkeys, writes=[('ps', bu)])
                    gi = f % 2
                    act(sgt[:, gi, :], ps[:, bg, :], AF.Silu, [('ps', bg)], [('sg', gi)])
                    tt('dve', actb[:, q % 2, i, :], ps[:, bu, :], sgt[:, gi, :], ALU.mult,
                       [('ps', bu), ('sg', gi)], [('act', q % 2, i)])

            def DN(q):
                akeys = [('act', q % 2, i) for i in range(FPP)]
                for dc in range(KC):
                    sd = load_slot(dk, l, q * 16 + dc)
                    b = rotA()
                    sch.mm(ps[:, b, :], [(slots[:, sd, i * 128:(i + 1) * 128], actb[:, q % 2, i, :]) for i in range(FPP)],
                           reads=[('slot', sd)] + akeys, writes=[('ps', b)])
                    stt(hT[:, dc, :], ps[:, b, :], 0.5, hT[:, dc, :], ALU.mult, ALU.add,
                        [('ps', b), ('h', dc)], [('h', dc)])

            GU(0)
            GU(1)
            DN(0)
            GU(2)
            DN(1)
            GU(3)
            DN(2)
            DN(3)
            if staged:
                flush_wb(0)

        def mix(l, s, t):
            base = l * NPL
            for sub in range(T // TM):
                c0 = sub * TM
                g0 = t * T + c0
                sbi = g0 // TM
                sch.dma('sp', out=cs, in_=cst_d[:, g0:g0 + TM], reads=[], writes=['cs'], semname='tab')
                bgq = []

                def bg(n):
                    for _ in range(min(n, len(bgq))):
                        bgq.pop(0)()

                def win_chunk(idx):
                    sl = load_slot_plain('win', l, idx)
                    b = rotA()
                    sch.mm(ps[:, b, :TM],
                           [(slots[:, sl, kc * 128:(kc + 1) * 128], xn[:, kc, c0:c0 + TM]) for kc in range(KC)],
                           reads=[('slot', sl)] + xkeys, writes=[('ps', b)])
                    return b

                if g0 == 0:
                    for j in range(4):
                        sch.op('dve', 'memset', dict(ap=ubuf[:, j, 0:30], constant=0.0), [], [('ub', j)])
                        sch.op('dve', 'memset', dict(ap=mbuf[:, j, 0:2], constant=0.0), [], [('mb', j)])
                for j in range(4):
                    ba = win_chunk(9 + j)
                    bg_ = win_chunk(13 + j)
                    sg_ap, sg_key = tmp()
                    act(sg_ap, ps[:, bg_, :TM], AF.Sigmoid, [('ps', bg_)], [sg_key])
                    tt('dve', ubuf[:, j, 30:30 + TM], ps[:, ba, :TM], sg_ap, ALU.mult,
                       [('ps', ba), sg_key], [('ub', j)])

                def conv_tap(k, j):
                    wcol = base + 56 + j * 31 + k
                    if k == 0:
                        ts('dve', acc[:, j, :], ubuf[:, j, 0:TM], pc(wcol), pc(base + 180 + j), ALU.mult, ALU.add,
                           [('ub', j), 'par'], [('acc', j)])
                    else:
                        stt(acc[:, j, :], ubuf[:, j, k:k + TM], pc(wcol), acc[:, j, :], ALU.mult, ALU.add,
                            [('ub', j), ('acc', j), 'par'], [('acc', j)])

                def halo(j):
                    sch.op('dve', 'tensor_copy', dict(out=ubuf[:, j, 0:30], in_=ubuf[:, j, TM:TM + 30]),
                           [('ub', j)], [('ub', j)])

                for k in range(31):
                    for j in range(4):
                        bgq.append(lambda k=k, j=j: conv_tap(k, j))
                for j in range(4):
                    bgq.append(lambda j=j: halo(j))

                def latent(idx0, gcol, dst, sbank, use_tmp=False):
                    if use_tmp:
                        cb = [tmp() for _ in range(4)]
                    else:
                        cb = [(c32[:, j, :], ('c32', j)) for j in range(4)]
                    for j in range(4):
                        b = win_chunk(idx0 + j)
                        act(cb[j][0], ps[:, b, :TM], AF.Copy, [('ps', b)], [cb[j][1]])
                        qi = rotQ()
                        act(sqb[:, qi, :TM], ps[:, b, :TM], AF.Square, [('ps', b)], [('sqb', qi)])
                        sch.mm(ps[:, sbank, :TM], [(ones[:, :], sqb[:, qi, :TM])], reads=[('sqb', qi), 'ones'],
                               writes=[('ps', sbank)], start=(j == 0), stop=(j == 3))
                        bg(3)
                    r_ap, r_key = rstd_from(ps[:, sbank, :TM], ('ps', sbank), TM, 1.0 / 512, RMS_EPS)
                    for j in range(4):
                        stt(cn[:, dst, j, :], cb[j][0], pc(gcol + j), r_ap, ALU.mult, ALU.mult,
                            [cb[j][1], r_key, 'par'], [('cn', dst, j)])

                mixstate['abanks'] = 6
                latent(0, base + 48, 0, 6)
                latent(4, base + 52, 1, 7, use_tmp=True)
                b = win_chunk(8)
                pk_ap, pk_key = tmp()
                tt('dve', pk_ap, ps[:, b, :TM], cs, ALU.mult, [('ps', b), 'cs'], [pk_key])
                b2 = rotA()
                sch.mm(ps[:, b2, :TM], [(dmat[:, :], pk_ap)], reads=[pk_key, 'dmat'], writes=[('ps', b2)])
                act(krT[:, g0:g0 + TM], ps[:, b2, :TM], AF.Copy, [('ps', b2)], [('kr', sbi)])
                cqk = [('cn', 0, kc) for kc in range(4)]
                for sl4 in range(4):
                    su = load_slot_plain('uq', l, sl4)
                    for hh in range(2):
                        h = sl4 * 2 + hh
                        b = rotA()
                        o = hh * 256
                        sch.mm(ps[:, b, :TM],
                               [(slots[:, su, kc * 512 + o:kc * 512 + o + 128], cn[:, 0, kc, :]) for kc in range(4)],
                               reads=[('slot', su)] + cqk, writes=[('ps', b)])
                        act(qT[:, h, :], ps[:, b, :TM], AF.Copy, [('ps', b)], [('q', h)])
                        b = rotA()
                        sch.mm(ps[:, b, :TM],
                               [(slots[:, su, kc * 512 + o + 128:kc * 512 + o + 256], cn[:, 0, kc, :]) for kc in range(4)],
                               reads=[('slot', su)] + cqk, writes=[('ps', b)])
                        tt('dve', qrT[:, h, :], ps[:, b, :TM], cs, ALU.mult, [('ps', b), 'cs'], [('qr', h)])
                        bg(2)

                for j in range(4):
                    bc = win_chunk(17 + j)
                    bx = win_chunk(21 + j)
                    cca, cck = tmp()
                    act(cca, ps[:, bc, :TM], AF.Copy, [('ps', bc)], [cck])
                    tt('dve', mbuf[:, j, 2:2 + TM], ps[:, bx, :TM], cca, ALU.mult, [('ps', bx), cck], [('mb', j)])
                    bb = win_chunk(25 + j)
                    a2, a2k = tmp()
                    wc = base + 192 + j * 3
                    ts('dve', a2, mbuf[:, j, 0:TM], pc(wc), None, ALU.mult, None, [('mb', j), 'par'], [a2k])
                    bg(1)
                    stt(a2, mbuf[:, j, 1:1 + TM], pc(wc + 1), a2, ALU.mult, ALU.add, [('mb', j), a2k, 'par'], [a2k])
                    bg(1)
                    stt(a2, mbuf[:, j, 2:2 + TM], pc(wc + 2), a2, ALU.mult, ALU.add, [('mb', j), a2k, 'par'], [a2k])
                    bg(1)
                    soa, sok = tmp()
                    tt('dve', soa, ps[:, bb, :TM], a2, ALU.mult, [('ps', bb), a2k], [sok])
                    sch.op('dve', 'tensor_copy', dict(out=mbuf[:, j, 0:2], in_=mbuf[:, j, TM:TM + 2]),
                           [('mb', j)], [('mb', j)])
                    group_norm(soa, sok, 12 + j, base + 204 + 12 + j)
                    bg(2)

                ckk = [('cn', 1, kc) for kc in range(4)]
                for sl4 in range(4):
                    sv = load_slot_plain('ukv', l, sl4)
                    for hh in range(2):
                        h = sl4 * 2 + hh
                        b = rotA()
                        o = hh * 128
                        sch.mm(ps[:, b, :TM],
                               [(slots[:, sv, kc * 512 + o:kc * 512 + o + 128], cn[:, 1, kc, :]) for kc in range(4)],
                               reads=[('slot', sv)] + ckk, writes=[('ps', b)])
                        act(kT[:, h, g0:g0 + TM], ps[:, b, :TM], AF.Copy, [('ps', b)], [('k', sbi, h)])
                    for tb in range(TM // 128):
                        kb = g0 // 128 + tb
                        b = rotA()
                        sch.mm(ps[:, b, :256],
                               [(cn[:, 1, kc, tb * 128:(tb + 1) * 128], slots[:, sv, kc * 512 + 256:kc * 512 + 512])
                                for kc in range(4)],
                               reads=[('slot', sv)] + ckk, writes=[('ps', b)])
                        act(Vc[:, kb, sl4 * 256:(sl4 + 1) * 256], ps[:, b, :256], AF.Copy, [('ps', b)], [('v', kb, sl4)])
                        bg(2)

                nkb = (g0 + TM) // 128
                mixstate['abanks'] = 2
                pend = [None]

                def ln_stage0():
                    for j in range(4):
                        act(vb[:, j, :], acc[:, j, :], AF.Copy, [('acc', j)], [('vb', j)])
                        act(sqb[:, j, TM:2 * TM], acc[:, j, :], AF.Square, [('acc', j)], [('sqb', j)])
                        sch.mm(ps[:, 0, :TM], [(ones[:, :], vb[:, j, :])], reads=[('vb', j), 'ones'],
                               writes=[('ps', 0)], start=(j == 0), stop=(j == 3))
                        sch.mm(ps[:, 1, :TM], [(ones[:, :], sqb[:, j, TM:2 * TM])], reads=[('sqb', j), 'ones'],
                               writes=[('ps', 1)], start=(j == 0), stop=(j == 3))

                def ln_stage1():
                    ts('dve', mean, ps[:, 0, :TM], 1.0 / 512, None, ALU.mult, None, [('ps', 0)], ['mean'])
                    tt('dve', c32[:, 0, :], mean, mean, ALU.mult, ['mean'], [('c32', 0)])
                    stt(c32[:, 1, :], ps[:, 1, :TM], 1.0 / 512, c32[:, 0, :], ALU.mult, ALU.subtract,
                        [('ps', 1), ('c32', 0)], [('c32', 1)])
                    act(c32[:, 2, :], c32[:, 1, :], AF.Ln, [('c32', 1)], [('c32', 2)], scale=1.0, bias=LN_EPS)
                    act(rstdl, c32[:, 2, :], AF.Exp, [('c32', 2)], ['rstdl'], scale=-0.5)

                def ln_stage2():
                    for j in range(4):
                        tt('dve', acc[:, j, :], acc[:, j, :], mean, ALU.subtract, [('acc', j), 'mean'], [('acc', j)])
                    for j in range(4):
                        tt('dve', acc[:, j, :], acc[:, j, :], rstdl, ALU.mult, [('acc', j), 'rstdl'], [('acc', j)])
                    for j in range(4):
                        act(acc[:, j, :], acc[:, j, :], AF.Silu, [('acc', j), 'par'], [('acc', j)],
                            scale=pc(base + 184 + j), bias=pc(base + 188 + j))

                def ln_stage3():
                    for j in range(4):
                        group_norm(acc[:, j, :], ('acc', j), 8 + j, base + 204 + 8 + j)

                for h in range(H):
                    ob, sbk = (6, 7) if h % 2 == 0 else (2, 3)

                    def S_(p):
                        b = rotS()
                        pi = rotP()
                        groups = []
                        info = []
                        rd = [('q', h), ('qr', h)]
                        for i in range(2):
                            kb = 2 * p + i
                            qlo = 0 if kb * 128 <= g0 else 128
                            N = TM - qlo
                            groups.append((ps[:, b, i * 256:i * 256 + N],
                                           [(kT[:, h, kb * 128:(kb + 1) * 128], qT[:, h, qlo:TM]),
                                            (krT[:, kb * 128:(kb + 1) * 128], qrT[:, h, qlo:TM])], True, True))
                            info.append((kb, qlo, N))
                            rd += [('k', kb // 2, h), ('kr', kb // 2)]
                        sch.mm_multi(groups, reads=rd, writes=[('ps', b)])
                        W_ = 256 + info[1][2]
                        act(pT2[:, pi, :W_], ps[:, b, :W_], AF.Exp, [('ps', b)], [('p', pi)], scale=SCALE)
                        for i in range(2):
                            if info[i][0] * 128 >= g0:
                                tt('dve', pT2[:, pi, i * 256:i * 256 + 128], pT2[:, pi, i * 256:i * 256 + 128],
                                   mask[:, :], ALU.mult, [('p', pi), 'mask'], [('p', pi)])
                        return (pi, info)

                    def PV(pi, info, first, last):
                        groups = []
                        rd = [('p', pi), 'ones']
                        for i in range(2):
                            kb, qlo, N = info[i]
                            st_ = first and i == 0
                            sp_ = last and i == 1
                            groups.append((ps[:, ob, qlo:qlo + N],
                                           [(Vc[:, kb, h * 128:(h + 1) * 128], pT2[:, pi, i * 256:i * 256 + N])],
                                           st_, sp_))
                            groups.append((ps[:, sbk, qlo:qlo + N],
                                           [(ones[:, :], pT2[:, pi, i * 256:i * 256 + N])], st_, sp_))
                            rd.append(('v', kb, h // 2))
                        sch.mm_multi(groups, reads=rd, writes=[('ps', ob), ('ps', sbk)])

                    npair = nkb // 2
                    prev = S_(0)
                    for p in range(npair):
                        nxt = S_(p + 1) if p + 1 < npair else None
                        PV(prev[0], prev[1], p == 0, p == npair - 1)
                        prev = nxt
                    if pend[0] is not None:
                        pend[0]()
                        pend[0] = None
                    ra, rk = tmp()
                    sch.op('dve', 'reciprocal', dict(out=ra, in_=ps[:, sbk, :TM]), [('ps', sbk)], [rk])
                    oa, ok = tmp()
                    tt('dve', oa, ps[:, ob, :TM], ra, ALU.mult, [('ps', ob), rk], [ok])
                    mark = rot['T']
                    pend[0] = (lambda oa=oa, ok=ok, h=h, mark=mark: (chk_tmp(mark), group_norm(oa, ok, h, base + 204 + h)))
                    bg(16)
                    if h == 4:
                        bg(len(bgq))
                        ln_stage0()
                        ln_stage1()
                    elif h == 5:
                        ln_stage2()
                    elif h == 6:
                        ln_stage3()
                pend[0]()
                mixstate['abanks'] = 4

                ykeys = [('y', kc) for kc in range(KC)]
                for dc in range(KC):
                    so_ = load_slot_plain('wo', l, dc)
                    b = rotA()
                    sch.mm(ps[:, b, :TM], [(slots[:, so_, kc * 128:(kc + 1) * 128], yT[:, kc, :]) for kc in range(KC)],
                           reads=[('slot', so_)] + ykeys, writes=[('ps', b)])
                    tt('dve', hT[:, dc, c0:c0 + TM], ps[:, b, :TM], hT[:, dc, c0:c0 + TM], ALU.add,
                       [('ps', b), ('h', dc)], [('h', dc)])

        sch.op('dve', 'memset', dict(ap=ones[:, :], constant=1.0), [], ['ones'])
        sch.dma('pool', out=par[:, :], in_=par_d[:, :], writes=['par'], semname='c0')
        sch.dma('pool', out=mask[:, :], in_=msk_d[:, :], writes=['mask'], semname='c1')
        sch.dma('pool', out=dmat[:, :], in_=dmat_d[:, :], writes=['dmat'], semname='c2')

        def load_h(l, s, t):
            src = xT if l == 0 else hscr
            for kc in range(KC):
                r0 = s * D + kc * 128
                sch.dma('act', out=hT[:, kc, :], in_=src[r0:r0 + 128, t * T:(t + 1) * T],
                        reads=([('hd', s, t, kc)] if l > 0 else []), writes=[('h', kc)], semname="hio%d" % (kc % 4))

        def store_h(dst, s, t):
            for kc in range(KC):
                r0 = s * D + kc * 128
                sch.dma('act', out=dst[r0:r0 + 128, t * T:(t + 1) * T], in_=hT[:, kc, :],
                        reads=[('h', kc)], writes=[('hd', s, t, kc)], semname="hio%d" % (kc % 4))

        load_h(0, 0, 0)
        cast_weights(0)
        first = True
        ntile = [0]
        for l in range(nL):
            for s in range(nS):
                if not first:
                    sch.new_epoch()
                for t in range(NT):
                    if not first:
                        load_h(l, s, t)
                    if ntile[0] == 1 and nL > 1:
                        for ll in range(1, nL):
                            cast_weights(ll)
                    ntile[0] += 1
                    stg_tile = (s == 0 and t == 0)
                    nxt_stg = (s == nS - 1 and t == NT - 1 and l + 1 < nL)
                    rmsnorm_D(l * NPL + 0)
                    ffn(l, 1, staged=stg_tile)
                    rmsnorm_D(l * NPL + 16)
                    sch.barrier()
                    mix(l, s, t)
                    rmsnorm_D(l * NPL + 32)
                    sch.barrier(extra=(('sp',) if (stg_tile or nxt_stg) else ()))
                    ffn(l, 2, staged=stg_tile)
                    if l == nL - 1:
                        fcol = nL * NPL
                        for kc in range(KC):
                            qi = rotQ()
                            act(sqb[:, qi, :], hT[:, kc, :], AF.Square, [('h', kc)], [('sqb', qi)])
                            sch.mm(ps[:, 6, :], [(ones[:, :], sqb[:, qi, :])], reads=[('sqb', qi), 'ones'],
                                   writes=[('ps', 6)], start=(kc == 0), stop=(kc == KC - 1))
                        r_ap, r_key = rstd_from(ps[:, 6, :], ('ps', 6), T, 1.0 / D, RMS_EPS)
                        for kc in range(KC):
                            stt(hT[:, kc, :], hT[:, kc, :], pc(fcol + kc), r_ap, ALU.mult, ALU.mult,
                                [('h', kc), r_key, 'par'], [('h', kc)])
                        store_h(outT, s, t)
                    else:
                        store_h(hscr, s, t)
                    first = False
        sch.wait_all_dma('pool')
        build.stats = dict(ninst=sch.ninst, nwait=sch.nwait, nsem=sch.nsem, cnt=dict(sch.cnt))
    return nc


def _slotify(W):
    nl, K, F = W.shape
    a = W.reshape(nl, K // 128, 128, F // 128, 128).transpose(0, 3, 2, 1, 4)
    return np.ascontiguousarray(a).reshape(nl * (F // 128) * 128, K)


def _slotify_down(W):
    nl = W.shape[0]
    a = W.reshape(nl, NPART, FPP, 128, KC, 128).transpose(0, 1, 4, 3, 2, 5)
    return np.ascontiguousarray(a).reshape(nl * 64 * 128, FPP * 128)


def _slotify_up(W):
    nl = W.shape[0]
    a = W.reshape(nl, 4, 128, 4, 512).transpose(0, 3, 2, 1, 4)
    return np.ascontiguousarray(a).reshape(nl * 4 * 128, 2048)


def prep_shared(inp, nL, S):
    f = lambda k: np.asarray(inp[k], dtype=np.float32)
    out = {}
    out['w_g1'] = _slotify(f('ffn1_w_gate')[:nL])
    out['w_u1'] = _slotify(f('ffn1_w_up')[:nL])
    out['w_d1'] = _slotify_down(f('ffn1_w_down')[:nL])
    out['w_g2'] = _slotify(f('ffn2_w_gate')[:nL])
    out['w_u2'] = _slotify(f('ffn2_w_up')[:nL])
    out['w_d2'] = _slotify_down(f('ffn2_w_down')[:nL])
    out['w_wo'] = _slotify(f('w_o')[:nL])
    r = np.arange
    cols = np.concatenate([r(0, 512), r(512, 1024), r(1024, 1088), r(1056, 1088), r(1024, 1056),
                           r(1088, 1600), r(1600, 2112), r(2624, 3136), r(3136, 3648), r(2112, 2624)])
    out['w_win'] = _slotify(f('w_in')[:nL][:, :, cols])
    cq = []
    for h in range(H):
        b = h * 192
        cq += [r(b, b + 128), r(b + 128, b + 192), r(b + 160, b + 192), r(b + 128, b + 160)]
    out['w_uq'] = _slotify_up(f('w_uq')[:nL][:, :, np.concatenate(cq)])
    ckv = []
    for sl in range(4):
        h0, h1 = 2 * sl, 2 * sl + 1
        ckv += [r(h0 * 256, h0 * 256 + 128), r(h1 * 256, h1 * 256 + 128),
                r(h0 * 256 + 128, h0 * 256 + 256), r(h1 * 256 + 128, h1 * 256 + 256)]
    out['w_ukv'] = _slotify_up(f('w_ukv')[:nL][:, :, np.concatenate(ckv)])
    NPAR = nL * NPL + 16
    par = np.zeros((128, NPAR), np.float32)

    def colsD(v):
        return v.reshape(KC, 128).T

    for l in range(nL):
        b = l * NPL
        par[:, b:b + 16] = colsD(f('ffn1_norm')[l])
        par[:, b + 16:b + 32] = colsD(f('mix_norm')[l])
        par[:, b + 32:b + 48] = colsD(f('ffn2_norm')[l])
        par[:, b + 48:b + 52] = f('q_norm')[l].reshape(4, 128).T
        par[:, b + 52:b + 56] = f('kv_norm')[l].reshape(4, 128).T
        cw = f('conf_dw_w')[l]
        for j in range(4):
            par[:, b + 56 + j * 31:b + 56 + (j + 1) * 31] = cw[:, j * 128:(j + 1) * 128].T
        par[:, b + 180:b + 184] = f('conf_dw_b')[l].reshape(4, 128).T
        par[:, b + 184:b + 188] = f('conf_ln_g')[l].reshape(4, 128).T
        par[:, b + 188:b + 192] = f('conf_ln_b')[l].reshape(4, 128).T
        sw = f('sc_w')[l]
        for j in range(4):
            par[:, b + 192 + j * 3:b + 192 + (j + 1) * 3] = sw[:, j * 128:(j + 1) * 128].T
        par[:, b + 204:b + 220] = f('mix_out_norm')[l].T
    par[:, nL * NPL:nL * NPL + 16] = colsD(f('final_norm'))
    out['par'] = par
    inv = (1.0 / (np.float32(10000.0) ** (np.arange(0, 64, 2, dtype=np.float32) / np.float32(64)))).astype(np.float32)
    ang = np.arange(S, dtype=np.float32)[None, :] * inv[:, None]
    c, sn = np.cos(ang).astype(np.float32), np.sin(ang).astype(np.float32)
    out['cst'] = np.ascontiguousarray(np.concatenate([c, c, -sn, sn], axis=0))
    k = np.arange(128)
    out['msk'] = (k[:, None] <= k[None, :]).astype(np.float32)
    out['dmat'] = ((k[:, None] % 64) == (k[None, :] % 64)).astype(np.float32)
    return out


_CACHE = {}


def run(inputs, nL, n_cores, nS, S, core_ids=None, trace=False):
    key = (nL, nS, S)
    if key not in _CACHE:
        _CACHE[key] = build(nL, nS, S)
    nc = _CACHE[key]
    shared = prep_shared(inputs, nL, S)
    x = np.asarray(inputs['x'], dtype=np.float32)
    in_maps = []
    for c in range(n_cores):
        xs = x[c * nS:(c + 1) * nS]
        m = dict(shared)
        m['xT'] = np.ascontiguousarray(xs.transpose(0, 2, 1)).reshape(nS * D, S)
        in_maps.append(m)
    res = run_bass_kernel_spmd(nc, in_maps, core_ids=(core_ids or list(range(n_cores))), trace=trace)
    outs = []
    for c in range(n_cores):
        o = np.asarray(res.results[c]['outT']).reshape(nS, D, S).transpose(0, 2, 1)
        outs.append(o)
    return np.ascontiguousarray(np.concatenate(outs, axis=0)).astype(np.float32), res


def kernel(**inputs):
    out, _ = run(inputs, nL=2, n_cores=8, nS=2, S=2048)
    return out
```

```python
import numpy as np
from contextlib import ExitStack
import concourse.bass as bass
import concourse.mybir as mybir
from concourse.bass_utils import run_bass_kernel_spmd

F32 = mybir.dt.float32
BF16 = mybir.dt.bfloat16
ALU = mybir.AluOpType
AF = mybir.ActivationFunctionType

D = 2048
KC = 16
DFF = 5632
FC = 44
NPART = 4
FPP = 11
T = 512
TM = 256
H = 8
NS = 6
RMS_EPS = 1e-6
LN_EPS = 1e-5
SCALE = float(192 ** -0.5)
NPL = 220

KINDS = {
    'g1': (44, 2048), 'u1': (44, 2048), 'd1': (64, 1408),
    'win': (29, 2048), 'uq': (4, 2048), 'ukv': (4, 2048), 'wo': (16, 2048),
    'g2': (44, 2048), 'u2': (44, 2048), 'd2': (64, 1408),
}
CAST_ORDER = ['g1', 'u1', 'd1', 'win', 'uq', 'ukv', 'wo', 'g2', 'u2', 'd2']


class Sched:
    ENG = ('pe', 'act', 'dve', 'pool')

    def __init__(self, nc, stack):
        self.nc = nc
        self.stack = stack
        self.h = dict(pe=nc.tensor, act=nc.scalar, dve=nc.vector, pool=nc.gpsimd, sp=nc.sync)
        self.cnt = {}
        self.sem = {}
        self.semkey = {}
        self.seen = {e: {} for e in self.h}
        self.res = {}
        self.nsem = 0
        self.dsem = {}
        self.ninst = 0
        self.nwait = 0
        self.new_epoch()

    def _newsem(self, name):
        s = self.stack.enter_context(self.nc.semaphore(name))
        self.nsem += 1
        return s

    def new_epoch(self):
        for e in self.ENG:
            nm = "%s_e%d" % (e, self.nsem)
            self.sem[e] = self._newsem(nm)
            self.semkey[e] = nm
            self.cnt[e] = 0

    def _wait(self, eng, stamp):
        sem, key, val, peng = stamp
        if peng == eng:
            if eng == 'pe':
                return
        if self.seen[eng].get(key, 0) >= val:
            return
        self.h[eng].wait_ge(sem, val)
        self.nwait += 1
        self.seen[eng][key] = val

    def _deps(self, eng, reads, writes):
        for r in reads:
            st = self.res.get(r)
            if st is not None and st[0] is not None:
                self._wait(eng, st[0])
        for w in writes:
            st = self.res.get(w)
            if st is not None:
                if st[0] is not None:
                    self._wait(eng, st[0])
                for s in st[1].values():
                    self._wait(eng, s)

    def _commit(self, stamp, rkey, reads, writes):
        for r in reads:
            st = self.res.get(r)
            if st is None:
                st = [None, {}]
                self.res[r] = st
            st[1][rkey] = stamp
        for w in writes:
            self.res[w] = [stamp, {}]

    def _stamp(self, eng):
        return (self.sem[eng], self.semkey[eng], self.cnt[eng], eng)

    def op(self, eng, name, kw, reads=(), writes=()):
        self._deps(eng, reads, writes)
        ins = getattr(self.h[eng], name)(**kw)
        self.cnt[eng] += 1
        ins.then_inc(self.sem[eng], 1)
        self.ninst += 1
        self._commit(self._stamp(eng), eng, reads, writes)

    def mm(self, out, pairs, reads=(), writes=(), start=True, stop=True):
        self._deps('pe', reads, writes)
        n = len(pairs)
        ins = None
        for i, (l, r) in enumerate(pairs):
            ins = self.nc.tensor.matmul(out, l, r, start=(start and i == 0), stop=(stop and i == n - 1))
        self.cnt['pe'] += 1
        ins.then_inc(self.sem['pe'], 1)
        self.ninst += n
        self._commit(self._stamp('pe'), 'pe', reads, writes)

    def mm_multi(self, groups, reads=(), writes=()):
        self._deps('pe', reads, writes)
        ins = None
        for out, pairs, start, stop in groups:
            n = len(pairs)
            for i, (l, r) in enumerate(pairs):
                ins = self.nc.tensor.matmul(out, l, r, start=(start and i == 0), stop=(stop and i == n - 1))
                self.ninst += 1
        self.cnt['pe'] += 1
        ins.then_inc(self.sem['pe'], 1)
        self._commit(self._stamp('pe'), 'pe', reads, writes)

    def dma(self, q, out, in_, reads=(), writes=(), semname='d', serialize=True, commit=True):
        ds = self.dsem.get(semname)
        if ds is None:
            ds = [self._newsem(semname), 0]
            self.dsem[semname] = ds
        if serialize and ds[1] > 0:
            self._wait(q, (ds[0], semname, ds[1], 'dma'))
        self._deps(q, reads, writes)
        ins = self.h[q].dma_start(out=out, in_=in_)
        ds[1] += 16
        ins.then_inc(ds[0], 16)
        self.ninst += 1
        stamp = (ds[0], semname, ds[1], 'dma')
        if commit:
            self._commit(stamp, 'dma:' + semname, reads, writes)
        return stamp

    def barrier(self, extra=()):
        engs = ('pe', 'act', 'dve')
        stamps = {e: self._stamp(e) for e in engs}
        for e in engs + tuple(extra):
            for e2 in engs:
                if e2 != e and stamps[e2][2] > 0:
                    self._wait(e, stamps[e2])

    def wait_all_dma(self, eng):
        for name, ds in self.dsem.items():
            if ds[1] > 0:
                self._wait(eng, (ds[0], name, ds[1], 'dma'))


def build(nL, nS, S):
    nc = bass.Bass("TRN2", target_bir_lowering=False)
    NT = S // T
    NKB = S // 128
    NPAR = nL * NPL + 16

    xT = nc.dram_tensor("xT", [nS * D, S], F32, kind="ExternalInput").ap()
    outT = nc.dram_tensor("outT", [nS * D, S], F32, kind="ExternalOutput").ap()
    hscr = nc.dram_tensor("hscr", [nS * D, S], F32, kind="Internal").ap()
    wsrc, wdst = {}, {}
    for k, (nsl, W) in KINDS.items():
        wsrc[k] = nc.dram_tensor("w_" + k, [nL * nsl * 128, W], F32, kind="ExternalInput").ap()
        wdst[k] = nc.dram_tensor("b_" + k, [nL * nsl * 128, W], BF16, kind="Internal").ap()
    par_d = nc.dram_tensor("par", [128, NPAR], F32, kind="ExternalInput").ap()
    cst_d = nc.dram_tensor("cst", [128, S], F32, kind="ExternalInput").ap()
    msk_d = nc.dram_tensor("msk", [128, 128], F32, kind="ExternalInput").ap()
    dmat_d = nc.dram_tensor("dmat", [128, 128], F32, kind="ExternalInput").ap()

    with ExitStack() as stack:
        def sb(name, shape, dt):
            return stack.enter_context(nc.sbuf_tensor(name, shape, dt))

        hT = sb("hT", [128, KC, T], F32)
        xn = sb("xn", [128, KC, T], BF16)
        slots = sb("slots", [128, NS, 2048], BF16)
        kT = sb("kT", [128, H, S], BF16)
        krT = sb("krT", [128, S], BF16)
        Vc = sb("Vc", [128, NKB, H * 128], BF16)
        sqb = sb("sqb", [128, 4, T], BF16)
        lnt = sb("lnt", [128, 2, T], F32)
        rstd = sb("rstd", [128, 2, T], F32)
        par = sb("par_sb", [128, NPAR], F32)
        ones = sb("ones", [128, 128], BF16)
        mask = sb("mask", [128, 128], BF16)
        dmat = sb("dmat_sb", [128, 128], F32)
        UW = 13312
        U = sb("U", [128, UW], F32)
        ps = stack.enter_context(nc.psum_tensor("ps", [128, 8, 512], F32))

        actb = U[:, 0:5632].bitcast(BF16).rearrange("p (a b c) -> p a b c", a=2, b=FPP)
        sgt = U[:, 5632:6656].rearrange("p (a b) -> p a b", b=T)
        stg = U[:, 6656:10752].rearrange("p (a b) -> p a b", b=1024)
        off = [0]

        def carve(ncols_f32):
            a = off[0]
            off[0] += ncols_f32
            assert off[0] <= UW, off[0]
            return U[:, a:a + ncols_f32]

        cs = carve(TM)
        c32 = carve(4 * TM).rearrange("p (a b) -> p a b", b=TM)
        cn = carve(4 * TM).bitcast(BF16).rearrange("p (a b c) -> p a b c", a=2, b=4)
        qT = carve(4 * TM).bitcast(BF16).rearrange("p (a b) -> p a b", b=TM)
        qrT = carve(4 * TM).bitcast(BF16).rearrange("p (a b) -> p a b", b=TM)
        pT2 = carve(2 * TM).bitcast(BF16).rearrange("p (a b) -> p a b", b=2 * TM)
        yT = carve(8 * TM).bitcast(BF16).rearrange("p (a b) -> p a b", b=TM)
        acc = carve(4 * TM).rearrange("p (a b) -> p a b", b=TM)
        vb = carve(2 * TM).bitcast(BF16).rearrange("p (a b) -> p a b", b=TM)
        NTMP = 8
        tmpt = carve(NTMP * TM).rearrange("p (a b) -> p a b", b=TM)
        rstdl = carve(TM)
        assert off[0] <= 10752, off[0]
        off[0] = 10752
        ubuf = carve(4 * (TM + 32)).rearrange("p (a b) -> p a b", b=TM + 32)
        mbuf = carve(4 * (TM + 4)).rearrange("p (a b) -> p a b", b=TM + 4)
        mean = carve(TM)

        sch = Sched(nc, stack)

        rot = {'A': 0, 'S': 0, 'P': 0, 'T': 0, 'Q': 0, 'R': 0, 'slot': 0, 'G': 0, 'S2': 0}

        mixstate = {'abanks': 4}

        ABANKS = {4: (0, 1, 2, 3), 2: (0, 1), 6: (0, 1, 2, 3, 4, 5)}

        def rotA():
            lst = ABANKS[mixstate['abanks']]
            b = lst[rot['A'] % len(lst)]
            rot['A'] += 1
            return b

        def chk_tmp(mark):
            assert rot['T'] - mark < NTMP - 1, (rot['T'], mark)

        def rotS():
            b = 4 + rot['S'] % 2
            rot['S'] += 1
            return b

        def rotP():
            b = rot['P'] % 2
            rot['P'] += 1
            return b

        def tmp():
            i = rot['T'] % NTMP
            rot['T'] += 1
            return tmpt[:, i, :], ('tmp', i)

        def rotR():
            i = rot['R'] % 2
            rot['R'] += 1
            return i

        def rotQ():
            i = rot['Q'] % 4
            rot['Q'] += 1
            return i

        def act(out, in_, func, reads, writes, **kw):
            sch.op('act', 'activation', dict(out=out, in_=in_, func=func, **kw), reads, writes)

        def tt(eng, out, in0, in1, op, reads, writes):
            sch.op(eng, 'tensor_tensor', dict(out=out, in0=in0, in1=in1, op=op), reads, writes)

        def stt(out, in0, scalar, in1, op0, op1, reads, writes):
            sch.op('dve', 'scalar_tensor_tensor',
                   dict(out=out, in0=in0, scalar=scalar, in1=in1, op0=op0, op1=op1), reads, writes)

        def ts(eng, out, in0, s1, s2, op0, op1, reads, writes):
            kw = dict(out=out, in0=in0, scalar1=s1, scalar2=s2, op0=op0)
            if s2 is not None:
                kw['op1'] = op1
            sch.op(eng, 'tensor_scalar', kw, reads, writes)

        def pc(c):
            return par[:, c:c + 1]

        def load_slot_plain(kind, l, idx):
            nsl, W = KINDS[kind]
            si = rot['slot'] % NS
            rot['slot'] += 1
            r0 = (l * nsl + idx) * 128
            if (kind, l, idx) in store_stamp:
                sch._wait('sp', store_stamp[(kind, l, idx)])
            else:
                sch._wait('sp', cast_stamp[(kind, l, (idx * 128) // 2048)])
            sch.dma('sp', out=slots[:, si, 0:W], in_=wdst[kind][r0:r0 + 128, :],
                    reads=[], writes=[('slot', si)], semname="slot%d" % si)
            return si

        store_stamp = {}
        pend_wb = []
        STAGED = ('g1', 'u1', 'd1', 'g2', 'u2', 'd2')

        def flush_wb(keep):
            while len(pend_wb) > keep:
                kind, l, idx, si, W, r0 = pend_wb.pop(0)
                store_stamp[(kind, l, idx)] = sch.dma(
                    'sp', out=wdst[kind][r0:r0 + 128, :], in_=slots[:, si, 0:W],
                    reads=[('slot', si)], writes=[], semname="wb%d" % (rot['G'] % 3))
                rot['G'] += 1

        def load_slot_staged(kind, l, idx):
            nsl, W = KINDS[kind]
            si = rot['slot'] % NS
            rot['slot'] += 1
            r0 = (l * nsl + idx) * 128
            hw = W // 2
            for hf in range(2):
                gi = rot['S2'] % 4
                rot['S2'] += 1
                c0_ = hf * hw
                sch.dma('sp', out=stg[:, gi, 0:hw], in_=wsrc[kind][r0:r0 + 128, c0_:c0_ + hw], reads=[],
                        writes=[('stg', gi)], semname="stg%d" % gi)
                act(slots[:, si, c0_:c0_ + hw], stg[:, gi, 0:hw], AF.Copy, [('stg', gi)], [('slot', si)])
            pend_wb.append((kind, l, idx, si, W, r0))
            flush_wb(2)
            return si

        ncast = [0]
        cast_stamp = {}

        def cast_dma(kind, l, di):
            nsl, W = KINDS[kind]
            R = nsl * 128
            base = l * R
            r0 = di * 2048
            n = min(2048, R - r0)
            ncast[0] += 1
            cast_stamp[(kind, l, di)] = sch.dma(
                'pool', out=wdst[kind][base + r0:base + r0 + n, :], in_=wsrc[kind][base + r0:base + r0 + n, :],
                semname="cast%d" % (ncast[0] % 7), serialize=True, commit=False)

        def ncd(kind):
            return (KINDS[kind][0] * 128 + 2047) // 2048

        def cast_weights(l):
            for kind in ('win', 'uq', 'ukv', 'wo'):
                for di in range(ncd(kind)):
                    cast_dma(kind, l, di)
            return
            for di in range(ncd('g1')):
                cast_dma('g1', l, di)
                cast_dma('u1', l, di)
            for di in range(ncd('d1')):
                cast_dma('d1', l, di)
            for kind in ('win', 'uq', 'ukv', 'wo'):
                for di in range(ncd(kind)):
                    cast_dma(kind, l, di)
            for di in range(ncd('g2')):
                cast_dma('g2', l, di)
                cast_dma('u2', l, di)
            for di in range(ncd('d2')):
                cast_dma('d2', l, di)

        def rstd_from(src_ap, src_key, ncols, scale, eps):
            i = rotR()
            act(lnt[:, i, :ncols], src_ap, AF.Ln, [src_key], [('lnt', i)], scale=scale, bias=eps)
            act(rstd[:, i, :ncols], lnt[:, i, :ncols], AF.Exp, [('lnt', i)], [('rstd', i)], scale=-0.5)
            return rstd[:, i, :ncols], ('rstd', i)

        def rmsnorm_D(gcol):
            for kc in range(KC):
                qi = rotQ()
                act(sqb[:, qi, :], hT[:, kc, :], AF.Square, [('h', kc)], [('sqb', qi)])
                sch.mm(ps[:, 6, :], [(ones[:, :], sqb[:, qi, :])], reads=[('sqb', qi), 'ones'],
                       writes=[('ps', 6)], start=(kc == 0), stop=(kc == KC - 1))
            r_ap, r_key = rstd_from(ps[:, 6, :], ('ps', 6), T, 1.0 / D, RMS_EPS)
            for kc in range(KC):
                stt(xn[:, kc, :], hT[:, kc, :], pc(gcol + kc), r_ap, ALU.mult, ALU.mult,
                    [('h', kc), r_key, 'par'], [('xn', kc)])

        def group_norm(src_ap, src_key, ycol, gcol):
            qi = rotQ()
            act(sqb[:, qi, :TM], src_ap, AF.Square, [src_key], [('sqb', qi)])
            b = rotA()
            sch.mm(ps[:, b, :TM], [(ones[:, :], sqb[:, qi, :TM])], reads=[('sqb', qi), 'ones'], writes=[('ps', b)])
            r_ap, r_key = rstd_from(ps[:, b, :TM], ('ps', b), TM, 1.0 / 128, RMS_EPS)
            stt(yT[:, ycol, :], src_ap, pc(gcol), r_ap, ALU.mult, ALU.mult,
                [src_key, r_key, 'par'], [('y', ycol)])

        xkeys = [('xn', kc) for kc in range(KC)]

        def ffn(l, which, staged=False):
            gk, uk, dk = ('g1', 'u1', 'd1') if which == 1 else ('g2', 'u2', 'd2')
            load_slot = load_slot_staged if staged else load_slot_plain

            def GU(q):
                for i in range(FPP):
                    f = q * FPP + i
                    sg_ = load_slot(gk, l, f)
                    bg = rotA()
                    sch.mm(ps[:, bg, :], [(slots[:, sg_, kc * 128:(kc + 1) * 128], xn[:, kc, :]) for kc in range(KC)],
                           reads=[('slot', sg_)] + xkeys, writes=[('ps', bg)])
                    su_ = load_slot(uk, l, f)
                    bu = rotA()
                    sch.mm(ps[:, bu, :], [(slots[:, su_, kc * 128:(kc + 1) * 128], xn[:, kc, :]) for kc in range(KC)],
                           reads=[('slot', su_)] + xkeys, writes=[('ps', bu)])
                    gi = f % 2
                    act(sgt[:, gi, :], ps[:, bg, :], AF.Silu, [('ps', bg)], [('sg', gi)])
                    tt('dve', actb[:, q % 2, i, :], ps[:, bu, :], sgt[:, gi, :], ALU.mult,
                       [('ps', bu), ('sg', gi)], [('act', q % 2, i)])

            def DN(q):
                akeys = [('act', q % 2, i) for i in range(FPP)]
                for dc in range(KC):
                    sd = load_slot(dk, l, q * 16 + dc)
                    b = rotA()
                    sch.mm(ps[:, b, :], [(slots[:, sd, i * 128:(i + 1) * 128], actb[:, q % 2, i, :]) for i in range(FPP)],
                           reads=[('slot', sd)] + akeys, writes=[('ps', b)])
                    stt(hT[:, dc, :], ps[:, b, :], 0.5, hT[:, dc, :], ALU.mult, ALU.add,
                        [('ps', b), ('h', dc)], [('h', dc)])

            GU(0)
            GU(1)
            DN(0)
            GU(2)
            DN(1)
            GU(3)
            DN(2)
            DN(3)
            if staged:
                flush_wb(0)

        def mix(l, s, t):
            base = l * NPL
            for sub in range(T // TM):
                c0 = sub * TM
                g0 = t * T + c0
                sbi = g0 // TM
                sch.dma('sp', out=cs, in_=cst_d[:, g0:g0 + TM], reads=[], writes=['cs'], semname='tab')
                bgq = []

                def bg(n):
                    for _ in range(min(n, len(bgq))):
                        bgq.pop(0)()

                def win_chunk(idx):
                    sl = load_slot_plain('win', l, idx)
                    b = rotA()
                    sch.mm(ps[:, b, :TM],
                           [(slots[:, sl, kc * 128:(kc + 1) * 128], xn[:, kc, c0:c0 + TM]) for kc in range(KC)],
                           reads=[('slot', sl)] + xkeys, writes=[('ps', b)])
                    return b

                if g0 == 0:
                    for j in range(4):
                        sch.op('dve', 'memset', dict(ap=ubuf[:, j, 0:30], constant=0.0), [], [('ub', j)])
                        sch.op('dve', 'memset', dict(ap=mbuf[:, j, 0:2], constant=0.0), [], [('mb', j)])
                for j in range(4):
                    ba = win_chunk(9 + j)
                    bg_ = win_chunk(13 + j)
                    sg_ap, sg_key = tmp()
                    act(sg_ap, ps[:, bg_, :TM], AF.Sigmoid, [('ps', bg_)], [sg_key])
                    tt('dve', ubuf[:, j, 30:30 + TM], ps[:, ba, :TM], sg_ap, ALU.mult,
                       [('ps', ba), sg_key], [('ub', j)])

                def conv_tap(k, j):
                    wcol = base + 56 + j * 31 + k
                    if k == 0:
                        ts('dve', acc[:, j, :], ubuf[:, j, 0:TM], pc(wcol), pc(base + 180 + j), ALU.mult, ALU.add,
                           [('ub', j), 'par'], [('acc', j)])
                    else:
                        stt(acc[:, j, :], ubuf[:, j, k:k + TM], pc(wcol), acc[:, j, :], ALU.mult, ALU.add,
                            [('ub', j), ('acc', j), 'par'], [('acc', j)])

                def halo(j):
                    sch.op('dve', 'tensor_copy', dict(out=ubuf[:, j, 0:30], in_=ubuf[:, j, TM:TM + 30]),
                           [('ub', j)], [('ub', j)])

                for k in range(31):
                    for j in range(4):
                        bgq.append(lambda k=k, j=j: conv_tap(k, j))
                for j in range(4):
                    bgq.append(lambda j=j: halo(j))

                def latent(idx0, gcol, dst, sbank, use_tmp=False):
                    if use_tmp:
                        cb = [tmp() for _ in range(4)]
                    else:
                        cb = [(c32[:, j, :], ('c32', j)) for j in range(4)]
                    for j in range(4):
                        b = win_chunk(idx0 + j)
                        act(cb[j][0], ps[:, b, :TM], AF.Copy, [('ps', b)], [cb[j][1]])
                        qi = rotQ()
                        act(sqb[:, qi, :TM], ps[:, b, :TM], AF.Square, [('ps', b)], [('sqb', qi)])
                        sch.mm(ps[:, sbank, :TM], [(ones[:, :], sqb[:, qi, :TM])], reads=[('sqb', qi), 'ones'],
                               writes=[('ps', sbank)], start=(j == 0), stop=(j == 3))
                        bg(3)
                    r_ap, r_key = rstd_from(ps[:, sbank, :TM], ('ps', sbank), TM, 1.0 / 512, RMS_EPS)
                    for j in range(4):
                        stt(cn[:, dst, j, :], cb[j][0], pc(gcol + j), r_ap, ALU.mult, ALU.mult,
                            [cb[j][1], r_key, 'par'], [('cn', dst, j)])

                mixstate['abanks'] = 6
                latent(0, base + 48, 0, 6)
                latent(4, base + 52, 1, 7, use_tmp=True)
                b = win_chunk(8)
                pk_ap, pk_key = tmp()
                tt('dve', pk_ap, ps[:, b, :TM], cs, ALU.mult, [('ps', b), 'cs'], [pk_key])
                b2 = rotA()
                sch.mm(ps[:, b2, :TM], [(dmat[:, :], pk_ap)], reads=[pk_key, 'dmat'], writes=[('ps', b2)])
                act(krT[:, g0:g0 + TM], ps[:, b2, :TM], AF.Copy, [('ps', b2)], [('kr', sbi)])
                cqk = [('cn', 0, kc) for kc in range(4)]
                for sl4 in range(4):
                    su = load_slot_plain('uq', l, sl4)
                    for hh in range(2):
                        h = sl4 * 2 + hh
                        b = rotA()
                        o = hh * 256
                        sch.mm(ps[:, b, :TM],
                               [(slots[:, su, kc * 512 + o:kc * 512 + o + 128], cn[:, 0, kc, :]) for kc in range(4)],
                               reads=[('slot', su)] + cqk, writes=[('ps', b)])
                        act(qT[:, h, :], ps[:, b, :TM], AF.Copy, [('ps', b)], [('q', h)])
                        b = rotA()
                        sch.mm(ps[:, b, :TM],
                               [(slots[:, su, kc * 512 + o + 128:kc * 512 + o + 256], cn[:, 0, kc, :]) for kc in range(4)],
                               reads=[('slot', su)] + cqk, writes=[('ps', b)])
                        tt('dve', qrT[:, h, :], ps[:, b, :TM], cs, ALU.mult, [('ps', b), 'cs'], [('qr', h)])
                        bg(2)

                sc_gn = []
                for j in range(4):
                    bc = win_chunk(17 + j)
                    bx = win_chunk(21 + j)
                    cca, cck = tmp()
                    act(cca, ps[:, bc, :TM], AF.Copy, [('ps', bc)], [cck])
                    tt('dve', mbuf[:, j, 2:2 + TM], ps[:, bx, :TM], cca, ALU.mult, [('ps', bx), cck], [('mb', j)])
                    bb = win_chunk(25 + j)
                    a2, a2k = tmp()
                    wc = base + 192 + j * 3
                    ts('dve', a2, mbuf[:, j, 0:TM], pc(wc), None, ALU.mult, None, [('mb', j), 'par'], [a2k])
                    bg(1)
                    stt(a2, mbuf[:, j, 1:1 + TM], pc(wc + 1), a2, ALU.mult, ALU.add, [('mb', j), a2k, 'par'], [a2k])
                    bg(1)
                    stt(a2, mbuf[:, j, 2:2 + TM], pc(wc + 2), a2, ALU.mult, ALU.add, [('mb', j), a2k, 'par'], [a2k])
                    bg(1)
                    tt('dve', c32[:, j, :], ps[:, bb, :TM], a2, ALU.mult, [('ps', bb), a2k], [('c32', j)])
                    sch.op('dve', 'tensor_copy', dict(out=mbuf[:, j, 0:2], in_=mbuf[:, j, TM:TM + 2]),
                           [('mb', j)], [('mb', j)])
                    sc_gn.append(lambda j=j: group_norm(c32[:, j, :], ('c32', j), 12 + j, base + 204 + 12 + j))
                    bg(2)

                ckk = [('cn', 1, kc) for kc in range(4)]
                for sl4 in range(4):
                    sv = load_slot_plain('ukv', l, sl4)
                    for hh in range(2):
                        h = sl4 * 2 + hh
                        b = rotA()
                        o = hh * 128
                        sch.mm(ps[:, b, :TM],
                               [(slots[:, sv, kc * 512 + o:kc * 512 + o + 128], cn[:, 1, kc, :]) for kc in range(4)],
                               reads=[('slot', sv)] + ckk, writes=[('ps', b)])
                        act(kT[:, h, g0:g0 + TM], ps[:, b, :TM], AF.Copy, [('ps', b)], [('k', sbi, h)])
                    for tb in range(TM // 128):
                        kb = g0 // 128 + tb
                        b = rotA()
                        sch.mm(ps[:, b, :256],
                               [(cn[:, 1, kc, tb * 128:(tb + 1) * 128], slots[:, sv, kc * 512 + 256:kc * 512 + 512])
                                for kc in range(4)],
                               reads=[('slot', sv)] + ckk, writes=[('ps', b)])
                        act(Vc[:, kb, sl4 * 256:(sl4 + 1) * 256], ps[:, b, :256], AF.Copy, [('ps', b)], [('v', kb, sl4)])
                        bg(2)

                nkb = (g0 + TM) // 128
                mixstate['abanks'] = 2
                pend = [None]

                def ln_stage0():
                    for j in range(4):
                        act(vb[:, j, :], acc[:, j, :], AF.Copy, [('acc', j)], [('vb', j)])
                        act(sqb[:, j, TM:2 * TM], acc[:, j, :], AF.Square, [('acc', j)], [('sqb', j)])
                        sch.mm(ps[:, 0, :TM], [(ones[:, :], vb[:, j, :])], reads=[('vb', j), 'ones'],
                               writes=[('ps', 0)], start=(j == 0), stop=(j == 3))
                        sch.mm(ps[:, 1, :TM], [(ones[:, :], sqb[:, j, TM:2 * TM])], reads=[('sqb', j), 'ones'],
                               writes=[('ps', 1)], start=(j == 0), stop=(j == 3))

                def ln_stage1():
                    ts('dve', mean, ps[:, 0, :TM], 1.0 / 512, None, ALU.mult, None, [('ps', 0)], ['mean'])
                    tt('dve', c32[:, 0, :], mean, mean, ALU.mult, ['mean'], [('c32', 0)])
                    stt(c32[:, 1, :], ps[:, 1, :TM], 1.0 / 512, c32[:, 0, :], ALU.mult, ALU.subtract,
                        [('ps', 1), ('c32', 0)], [('c32', 1)])
                    act(c32[:, 2, :], c32[:, 1, :], AF.Ln, [('c32', 1)], [('c32', 2)], scale=1.0, bias=LN_EPS)
                    act(rstdl, c32[:, 2, :], AF.Exp, [('c32', 2)], ['rstdl'], scale=-0.5)

                def ln_stage2():
                    for j in range(4):
                        tt('dve', acc[:, j, :], acc[:, j, :], mean, ALU.subtract, [('acc', j), 'mean'], [('acc', j)])
                    for j in range(4):
                        tt('dve', acc[:, j, :], acc[:, j, :], rstdl, ALU.mult, [('acc', j), 'rstdl'], [('acc', j)])
                    for j in range(4):
                        act(acc[:, j, :], acc[:, j, :], AF.Silu, [('acc', j), 'par'], [('acc', j)],
                            scale=pc(base + 184 + j), bias=pc(base + 188 + j))

                def ln_stage3():
                    for j in range(4):
                        group_norm(acc[:, j, :], ('acc', j), 8 + j, base + 204 + 8 + j)

                for h in range(H):
                    ob, sbk = (6, 7) if h % 2 == 0 else (2, 3)

                    def S_(p):
                        b = rotS()
                        pi = rotP()
                        groups = []
                        info = []
                        rd = [('q', h), ('qr', h)]
                        for i in range(2):
                            kb = 2 * p + i
                            qlo = 0 if kb * 128 <= g0 else 128
                            N = TM - qlo
                            groups.append((ps[:, b, i * 256:i * 256 + N],
                                           [(kT[:, h, kb * 128:(kb + 1) * 128], qT[:, h, qlo:TM]),
                                            (krT[:, kb * 128:(kb + 1) * 128], qrT[:, h, qlo:TM])], True, True))
                            info.append((kb, qlo, N))
                            rd += [('k', kb // 2, h), ('kr', kb // 2)]
                        sch.mm_multi(groups, reads=rd, writes=[('ps', b)])
                        W_ = 256 + info[1][2]
                        act(pT2[:, pi, :W_], ps[:, b, :W_], AF.Exp, [('ps', b)], [('p', pi)], scale=SCALE)
                        for i in range(2):
                            if info[i][0] * 128 >= g0:
                                tt('dve', pT2[:, pi, i * 256:i * 256 + 128], pT2[:, pi, i * 256:i * 256 + 128],
                                   mask[:, :], ALU.mult, [('p', pi), 'mask'], [('p', pi)])
                        return (pi, info)

                    def PV(pi, info, first, last):
                        groups = []
                        rd = [('p', pi), 'ones']
                        for i in range(2):
                            kb, qlo, N = info[i]
                            st_ = first and i == 0
                            sp_ = last and i == 1
                            groups.append((ps[:, ob, qlo:qlo + N],
                                           [(Vc[:, kb, h * 128:(h + 1) * 128], pT2[:, pi, i * 256:i * 256 + N])],
                                           st_, sp_))
                            groups.append((ps[:, sbk, qlo:qlo + N],
                                           [(ones[:, :], pT2[:, pi, i * 256:i * 256 + N])], st_, sp_))
                            rd.append(('v', kb, h // 2))
                        sch.mm_multi(groups, reads=rd, writes=[('ps', ob), ('ps', sbk)])

                    npair = nkb // 2
                    prev = S_(0)
                    for p in range(npair):
                        nxt = S_(p + 1) if p + 1 < npair else None
                        PV(prev[0], prev[1], p == 0, p == npair - 1)
                        prev = nxt
                    if pend[0] is not None:
                        pend[0]()
                        pend[0] = None
                    if h < 4:
                        sc_gn[h]()
                    ra, rk = tmp()
                    sch.op('dve', 'reciprocal', dict(out=ra, in_=ps[:, sbk, :TM]), [('ps', sbk)], [rk])
                    oa, ok = tmp()
                    tt('dve', oa, ps[:, ob, :TM], ra, ALU.mult, [('ps', ob), rk], [ok])
                    mark = rot['T']
                    pend[0] = (lambda oa=oa, ok=ok, h=h, mark=mark: (chk_tmp(mark), group_norm(oa, ok, h, base + 204 + h)))
                    bg(16)
                    if h == 4:
                        bg(len(bgq))
                        ln_stage0()
                        ln_stage1()
                    elif h == 5:
                        ln_stage2()
                    elif h == 6:
                        ln_stage3()
                pend[0]()
                mixstate['abanks'] = 4

                ykeys = [('y', kc) for kc in range(KC)]
                for dc in range(KC):
                    so_ = load_slot_plain('wo', l, dc)
                    b = rotA()
                    sch.mm(ps[:, b, :TM], [(slots[:, so_, kc * 128:(kc + 1) * 128], yT[:, kc, :]) for kc in range(KC)],
                           reads=[('slot', so_)] + ykeys, writes=[('ps', b)])
                    tt('dve', hT[:, dc, c0:c0 + TM], ps[:, b, :TM], hT[:, dc, c0:c0 + TM], ALU.add,
                       [('ps', b), ('h', dc)], [('h', dc)])

        sch.op('dve', 'memset', dict(ap=ones[:, :], constant=1.0), [], ['ones'])
        sch.dma('pool', out=par[:, :], in_=par_d[:, :], writes=['par'], semname='c0')
        sch.dma('pool', out=mask[:, :], in_=msk_d[:, :], writes=['mask'], semname='c1')
        sch.dma('pool', out=dmat[:, :], in_=dmat_d[:, :], writes=['dmat'], semname='c2')

        def load_h(l, s, t):
            src = xT if l == 0 else hscr
            for kc in range(KC):
                r0 = s * D + kc * 128
                sch.dma('act', out=hT[:, kc, :], in_=src[r0:r0 + 128, t * T:(t + 1) * T],
                        reads=([('hd', s, t, kc)] if l > 0 else []), writes=[('h', kc)], semname="hio%d" % (kc % 4))

        def store_h(dst, s, t):
            for kc in range(KC):
                r0 = s * D + kc * 128
                sch.dma('act', out=dst[r0:r0 + 128, t * T:(t + 1) * T], in_=hT[:, kc, :],
                        reads=[('h', kc)], writes=[('hd', s, t, kc)], semname="hio%d" % (kc % 4))

        load_h(0, 0, 0)
        cast_weights(0)
        first = True
        ntile = [0]
        for l in range(nL):
            for s in range(nS):
                if not first:
                    sch.new_epoch()
                for t in range(NT):
                    if not first:
                        load_h(l, s, t)
                    if ntile[0] == 1 and nL > 1:
                        for ll in range(1, nL):
                            cast_weights(ll)
                    ntile[0] += 1
                    stg_tile = (s == 0 and t == 0)
                    nxt_stg = (s == nS - 1 and t == NT - 1 and l + 1 < nL)
                    rmsnorm_D(l * NPL + 0)
                    ffn(l, 1, staged=stg_tile)
                    rmsnorm_D(l * NPL + 16)
                    sch.barrier()
                    mix(l, s, t)
                    rmsnorm_D(l * NPL + 32)
                    sch.barrier(extra=(('sp',) if (stg_tile or nxt_stg) else ()))
                    ffn(l, 2, staged=stg_tile)
                    if l == nL - 1:
                        fcol = nL * NPL
                        for kc in range(KC):
                            qi = rotQ()
                            act(sqb[:, qi, :], hT[:, kc, :], AF.Square, [('h', kc)], [('sqb', qi)])
                            sch.mm(ps[:, 6, :], [(ones[:, :], sqb[:, qi, :])], reads=[('sqb', qi), 'ones'],
                                   writes=[('ps', 6)], start=(kc == 0), stop=(kc == KC - 1))
                        r_ap, r_key = rstd_from(ps[:, 6, :], ('ps', 6), T, 1.0 / D, RMS_EPS)
                        for kc in range(KC):
                            stt(hT[:, kc, :], hT[:, kc, :], pc(fcol + kc), r_ap, ALU.mult, ALU.mult,
                                [('h', kc), r_key, 'par'], [('h', kc)])
                        store_h(outT, s, t)
                    else:
                        store_h(hscr, s, t)
                    first = False
        sch.wait_all_dma('pool')
        build.stats = dict(ninst=sch.ninst, nwait=sch.nwait, nsem=sch.nsem, cnt=dict(sch.cnt))
    return nc


def _slotify(W):
    nl, K, F = W.shape
    a = W.reshape(nl, K // 128, 128, F // 128, 128).transpose(0, 3, 2, 1, 4)
    return np.ascontiguousarray(a).reshape(nl * (F // 128) * 128, K)


def _slotify_down(W):
    nl = W.shape[0]
    a = W.reshape(nl, NPART, FPP, 128, KC, 128).transpose(0, 1, 4, 3, 2, 5)
    return np.ascontiguousarray(a).reshape(nl * 64 * 128, FPP * 128)


def _slotify_up(W):
    nl = W.shape[0]
    a = W.reshape(nl, 4, 128, 4, 512).transpose(0, 3, 2, 1, 4)
    return np.ascontiguousarray(a).reshape(nl * 4 * 128, 2048)


def prep_shared(inp, nL, S):
    f = lambda k: np.asarray(inp[k], dtype=np.float32)
    out = {}
    out['w_g1'] = _slotify(f('ffn1_w_gate')[:nL])
    out['w_u1'] = _slotify(f('ffn1_w_up')[:nL])
    out['w_d1'] = _slotify_down(f('ffn1_w_down')[:nL])
    out['w_g2'] = _slotify(f('ffn2_w_gate')[:nL])
    out['w_u2'] = _slotify(f('ffn2_w_up')[:nL])
    out['w_d2'] = _slotify_down(f('ffn2_w_down')[:nL])
    out['w_wo'] = _slotify(f('w_o')[:nL])
    r = np.arange
    cols = np.concatenate([r(0, 512), r(512, 1024), r(1024, 1088), r(1056, 1088), r(1024, 1056),
                           r(1088, 1600), r(1600, 2112), r(2624, 3136), r(3136, 3648), r(2112, 2624)])
    out['w_win'] = _slotify(f('w_in')[:nL][:, :, cols])
    cq = []
    for h in range(H):
        b = h * 192
        cq += [r(b, b + 128), r(b + 128, b + 192), r(b + 160, b + 192), r(b + 128, b + 160)]
    out['w_uq'] = _slotify_up(f('w_uq')[:nL][:, :, np.concatenate(cq)])
    ckv = []
    for sl in range(4):
        h0, h1 = 2 * sl, 2 * sl + 1
        ckv += [r(h0 * 256, h0 * 256 + 128), r(h1 * 256, h1 * 256 + 128),
                r(h0 * 256 + 128, h0 * 256 + 256), r(h1 * 256 + 128, h1 * 256 + 256)]
    out['w_ukv'] = _slotify_up(f('w_ukv')[:nL][:, :, np.concatenate(ckv)])
    NPAR = nL * NPL + 16
    par = np.zeros((128, NPAR), np.float32)

    def colsD(v):
        return v.reshape(KC, 128).T

    for l in range(nL):
        b = l * NPL
        par[:, b:b + 16] = colsD(f('ffn1_norm')[l])
        par[:, b + 16:b + 32] = colsD(f('mix_norm')[l])
        par[:, b + 32:b + 48] = colsD(f('ffn2_norm')[l])
        par[:, b + 48:b + 52] = f('q_norm')[l].reshape(4, 128).T
        par[:, b + 52:b + 56] = f('kv_norm')[l].reshape(4, 128).T
        cw = f('conf_dw_w')[l]
        for j in range(4):
            par[:, b + 56 + j * 31:b + 56 + (j + 1) * 31] = cw[:, j * 128:(j + 1) * 128].T
        par[:, b + 180:b + 184] = f('conf_dw_b')[l].reshape(4, 128).T
        par[:, b + 184:b + 188] = f('conf_ln_g')[l].reshape(4, 128).T
        par[:, b + 188:b + 192] = f('conf_ln_b')[l].reshape(4, 128).T
        sw = f('sc_w')[l]
        for j in range(4):
            par[:, b + 192 + j * 3:b + 192 + (j + 1) * 3] = sw[:, j * 128:(j + 1) * 128].T
        par[:, b + 204:b + 220] = f('mix_out_norm')[l].T
    par[:, nL * NPL:nL * NPL + 16] = colsD(f('final_norm'))
    out['par'] = par
    inv = (1.0 / (np.float32(10000.0) ** (np.arange(0, 64, 2, dtype=np.float32) / np.float32(64)))).astype(np.float32)
    ang = np.arange(S, dtype=np.float32)[None, :] * inv[:, None]
    c, sn = np.cos(ang).astype(np.float32), np.sin(ang).astype(np.float32)
    out['cst'] = np.ascontiguousarray(np.concatenate([c, c, -sn, sn], axis=0))
    k = np.arange(128)
    out['msk'] = (k[:, None] <= k[None, :]).astype(np.float32)
    out['dmat'] = ((k[:, None] % 64) == (k[None, :] % 64)).astype(np.float32)
    return out


_CACHE = {}


def run(inputs, nL, n_cores, nS, S, core_ids=None, trace=False):
    key = (nL, nS, S)
    if key not in _CACHE:
        _CACHE[key] = build(nL, nS, S)
    nc = _CACHE[key]
    shared = prep_shared(inputs, nL, S)
    x = np.asarray(inputs['x'], dtype=np.float32)
    in_maps = []
    for c in range(n_cores):
        xs = x[c * nS:(c + 1) * nS]
        m = dict(shared)
        m['xT'] = np.ascontiguousarray(xs.transpose(0, 2, 1)).reshape(nS * D, S)
        in_maps.append(m)
    res = run_bass_kernel_spmd(nc, in_maps, core_ids=(core_ids or list(range(n_cores))), trace=trace)
    outs = []
    for c in range(n_cores):
        o = np.asarray(res.results[c]['outT']).reshape(nS, D, S).transpose(0, 2, 1)
        outs.append(o)
    return np.ascontiguousarray(np.concatenate(outs, axis=0)).astype(np.float32), res


def kernel(**inputs):
    out, _ = run(inputs, nL=2, n_cores=8, nS=2, S=2048)
    return out
```
